# Optimizing a Trainium2 kernel written in Bass

```python
import math
import jax, jax.numpy as jnp
from jax import lax
import numpy as np

D_MODEL = 1024
BATCH = 8
SEQ = 8192
DEPTH = 4

N_HEADS = 16
N_KV_HEADS = 4
HEAD_DIM = 64
GROUP = N_HEADS // N_KV_HEADS
ATTN_WIDTH = N_HEADS * HEAD_DIM
KV_WIDTH = N_KV_HEADS * HEAD_DIM
WINDOW = 128
BLOCK = 128
ATTN_IN = 2 * ATTN_WIDTH + 2 * KV_WIDTH
POOL_WIDTH = D_MODEL
POOL_WINDOWS = (2, 4, 8, 16)
N_POOL_GROUPS = len(POOL_WINDOWS)
POOL_GC = POOL_WIDTH // N_POOL_GROUPS
POOL_IN = 2 * POOL_WIDTH
N_MIXERS = 2
N_A = (DEPTH + 1) // 2
N_B = DEPTH // 2
EPS = 1e-6

kernel_name = "hybrid_swa_sink_multiscale_pool"


def rmsnorm(x, g):
    xf = x.astype(jnp.float32)
    y = xf * lax.rsqrt(jnp.mean(xf * xf, axis=-1, keepdims=True) + EPS)
    return (y * g.astype(jnp.float32)).astype(x.dtype)


def swa_with_sinks(q, k, v, sinks):
    B, S = q.shape[0], q.shape[1]
    nb = S // BLOCK
    qb = q.reshape(B, nb, BLOCK, N_KV_HEADS, GROUP, HEAD_DIM)
    def band(t):
        tb = t.reshape(B, nb, BLOCK, N_KV_HEADS, HEAD_DIM)
        prev = jnp.pad(tb, ((0, 0), (1, 0), (0, 0), (0, 0), (0, 0)))[:, :-1]
        return jnp.concatenate([prev, tb], axis=2)
    kc, vc = band(k), band(v)
    s = jnp.einsum('bnqhgd,bnkhd->bnhgqk', qb, kc).astype(jnp.float32)
    s = s * (1.0 / math.sqrt(HEAD_DIM))
    qi = jnp.arange(BLOCK)[:, None]
    kj = jnp.arange(2 * BLOCK)[None, :]
    diff = qi + BLOCK - kj
    in_band = (diff >= 0) & (diff < WINDOW)
    not_pad = (jnp.arange(nb)[:, None, None] > 0) | (kj[None] >= BLOCK)
    valid = in_band[None] & not_pad
    s = jnp.where(valid[None, :, None, None], s, -jnp.inf)
    sink = sinks.astype(jnp.float32).reshape(N_KV_HEADS, GROUP)[:, :, None, None]
    m = jnp.maximum(jnp.max(s, axis=-1, keepdims=True), sink)
    p = jnp.exp(s - m)
    denom = jnp.sum(p, axis=-1, keepdims=True) + jnp.exp(sink - m)
    p = (p / denom).astype(v.dtype)
    o = jnp.einsum('bnhgqk,bnkhd->bnqhgd', p, vc)
    return o.reshape(B, S, ATTN_WIDTH)


def attn_layer(h, w_in, sinks, w_out):
    B, S, _ = h.shape
    proj = h @ w_in
    q, k, v, z = jnp.split(proj, [ATTN_WIDTH, ATTN_WIDTH + KV_WIDTH, ATTN_WIDTH + 2 * KV_WIDTH], axis=-1)
    q = q.reshape(B, S, N_KV_HEADS, GROUP, HEAD_DIM)
    k = k.reshape(B, S, N_KV_HEADS, HEAD_DIM)
    v = v.reshape(B, S, N_KV_HEADS, HEAD_DIM)
    o = swa_with_sinks(q, k, v, sinks)
    return (o * jax.nn.silu(z)) @ w_out


def multiscale_pool(u):
    B, S, _ = u.shape
    uf = u.astype(jnp.float32).reshape(B, S, N_POOL_GROUPS, POOL_GC)
    cs = jnp.pad(jnp.cumsum(uf, axis=1), ((0, 0), (1, 0), (0, 0), (0, 0)))
    outs = []
    for gi, w in enumerate(POOL_WINDOWS):
        c = cs[:, :, gi]
        lower = jnp.pad(c[:, :S + 1 - w], ((0, 0), (w - 1, 0), (0, 0)))
        count = jnp.minimum(jnp.arange(1, S + 1), w).astype(jnp.float32)[None, :, None]
        outs.append((c[:, 1:] - lower) / count - uf[:, :, gi])
    return jnp.stack(outs, axis=2).astype(u.dtype)


def pool_layer(h, w_in, w_mix, scale, w_out):
    B, S, _ = h.shape
    u, z = jnp.split(h @ w_in, [POOL_WIDTH], axis=-1)
    p = multiscale_pool(u)
    m = jnp.einsum('bsgc,gcd->bsgd', p, w_mix).reshape(B, S, POOL_WIDTH) * scale
    return (m * jax.nn.silu(z)) @ w_out


def setup_inputs(seed: int = 0) -> dict:
    key = jax.random.key(seed)
    ks = jax.random.split(key, 12)
    out_scale = 1.0 / math.sqrt(2.0 * DEPTH)
    x = jax.random.normal(ks[0], (BATCH, SEQ, D_MODEL), jnp.float32)
    norm_g = 1.0 + 0.05 * jax.random.normal(ks[1], (DEPTH, D_MODEL), jnp.float32)
    attn_w_in = jax.random.normal(ks[2], (N_A, D_MODEL, ATTN_IN), jnp.float32) * D_MODEL ** -0.5
    attn_sinks = 0.5 * jax.random.normal(ks[3], (N_A, N_HEADS), jnp.float32)
    attn_w_out = jax.random.normal(ks[4], (N_A, ATTN_WIDTH, D_MODEL), jnp.float32) * ATTN_WIDTH ** -0.5 * out_scale
    pool_w_in = jax.random.normal(ks[5], (N_B, D_MODEL, POOL_IN), jnp.float32) * D_MODEL ** -0.5
    pool_w_mix = jax.random.normal(ks[6], (N_B, N_POOL_GROUPS, POOL_GC, POOL_GC), jnp.float32) * POOL_GC ** -0.5
    pool_scale = 1.0 + 0.1 * jax.random.normal(ks[7], (N_B, POOL_WIDTH), jnp.float32)
    pool_w_out = jax.random.normal(ks[8], (N_B, POOL_WIDTH, D_MODEL), jnp.float32) * POOL_WIDTH ** -0.5 * out_scale
    final_g = 1.0 + 0.05 * jax.random.normal(ks[9], (D_MODEL,), jnp.float32)
    return {"x": x, "norm_g": norm_g, "attn_w_in": attn_w_in, "attn_sinks": attn_sinks,
            "attn_w_out": attn_w_out, "pool_w_in": pool_w_in, "pool_w_mix": pool_w_mix,
            "pool_scale": pool_scale, "pool_w_out": pool_w_out, "final_g": final_g}


def reference(x, norm_g, attn_w_in, attn_sinks, attn_w_out, pool_w_in, pool_w_mix,
              pool_scale, pool_w_out, final_g):
    for i in range(DEPTH):
        h = rmsnorm(x, norm_g[i])
        j = i // N_MIXERS
        if i % N_MIXERS == 0:
            y = attn_layer(h, attn_w_in[j], attn_sinks[j], attn_w_out[j])
        else:
            y = pool_layer(h, pool_w_in[j], pool_w_mix[j], pool_scale[j], pool_w_out[j])
        x = x + y.astype(x.dtype)
    return rmsnorm(x, final_g)
```

```python
from contextlib import ExitStack

import numpy as np
import ml_dtypes
import concourse.bass as bass
import concourse.mybir as mybir
from concourse.bass_utils import run_bass_kernel_spmd

F32 = mybir.dt.float32
BF16 = mybir.dt.bfloat16
ALU = mybir.AluOpType
AF = mybir.ActivationFunctionType

ENGS = ("pe", "act", "dve", "pool", "sp")

D = 1024
T = 512
NBLK = 4
WINDOWS = (2, 4, 8, 16)
EPS = 1e-6
N_CORES = 8


class Buf:
    __slots__ = ("name", "writer", "readers")

    def __init__(self, name):
        self.name = name
        self.writer = None
        self.readers = {}


class Op:
    __slots__ = ("eng", "fn", "deps", "signal", "sem", "inc", "ticket", "name")


class _Count:
    def __init__(self, e):
        self.e = e
        self.n = 0

    def matmul(self, *a, **k):
        self.n += 1
        return self.e.matmul(*a, **k)

    def transpose(self, *a, **k):
        self.n += 1
        return self.e.transpose(*a, **k)


class Sched:
    def __init__(self):
        self.q = {k: [] for k in ENGS}
        self.dsems = []
        self.ctx = ""
        self.pe_log = None

    def dsem(self, name):
        self.dsems.append(name)
        return name

    def add(self, eng, fn, reads=(), writes=(), dsem=None, name=""):
        name = self.ctx + name
        op = Op()
        op.eng = eng
        op.fn = fn
        op.signal = dsem is not None
        op.name = name
        op.sem = dsem if dsem is not None else eng
        op.inc = 16 if dsem is not None else 1
        op.ticket = None
        deps = []
        for b in reads:
            if b.writer is not None:
                deps.append(b.writer)
        for b in writes:
            if b.readers:
                deps.extend(b.readers.values())
            elif b.writer is not None:
                deps.append(b.writer)
        out = []
        for d in deps:
            if d is op:
                continue
            if d.sem == "pe" and eng == "pe":
                continue
            out.append(d)
            d.signal = True
        op.deps = out
        for b in reads:
            b.readers[op.sem] = op
        for b in writes:
            b.writer = op
            b.readers = {}
        self.q[eng].append(op)
        return op

    def emit(self, nc, final_waits=()):
        cnt = {}
        for eng in ENGS:
            for op in self.q[eng]:
                if op.signal:
                    cnt[op.sem] = cnt.get(op.sem, 0) + op.inc
                    op.ticket = cnt[op.sem]
        names = list(ENGS) + list(self.dsems)
        with ExitStack() as es:
            sems = {n: es.enter_context(nc.semaphore("s_" + n)) for n in names}
            block = es.enter_context(nc.Block())

            def run(engkey, e):
                waited = {}
                for op in self.q[engkey]:
                    need = {}
                    for d in op.deps:
                        if need.get(d.sem, 0) < d.ticket:
                            need[d.sem] = d.ticket
                    for s, v in need.items():
                        if waited.get(s, 0) < v:
                            e.wait_ge(sems[s], v)
                            waited[s] = v
                    if self.pe_log is not None and engkey == "pe":
                        cp = _Count(e)
                        inst = op.fn(cp)
                        self.pe_log.append((op.name, cp.n))
                    else:
                        inst = op.fn(e)
                    if op.signal:
                        inst.then_inc(sems[op.sem], op.inc)
                for (ek, op) in final_waits:
                    if ek == engkey:
                        e.wait_ge(sems[op.sem], op.ticket)

            @block.tensor
            def _(e):
                run("pe", e)

            @block.scalar
            def _(e):
                run("act", e)

            @block.vector
            def _(e):
                run("dve", e)

            @block.gpsimd
            def _(e):
                run("pool", e)

            @block.sync
            def _(e):
                run("sp", e)


def build_program(S_tok, kinds, do_final, seq_start=True, RS=6):
    nl = len(kinds)
    nA = max(1, sum(1 for k in kinds if k == "A"))
    nB = max(1, sum(1 for k in kinds if k == "B"))
    NT = S_tok // T
    NPAN = 7 * nl
    nc = bass.Bass("TRN2", target_bir_lowering=False)
    x_d = nc.dram_tensor("x", [S_tok, D], F32, kind="ExternalInput").ap()
    wpan_d = nc.dram_tensor("wpan", [NPAN, 128, 4096], F32, kind="ExternalInput").ap()
    gvec_d = nc.dram_tensor("gvec", [nl + 1, D], F32, kind="ExternalInput").ap()
    sinks_d = nc.dram_tensor("sinks", [128, nA * 8], F32, kind="ExternalInput").ap()
    pscale_d = nc.dram_tensor("pscale", [128, nB * 8], F32, kind="ExternalInput").ap()
    cbf_d = nc.dram_tensor("cbf", [128, 128 + 256 + 2560 + 256 + 1024], BF16, kind="ExternalInput").ap()
    out_d = nc.dram_tensor("out", [S_tok, D], F32, kind="ExternalOutput").ap()
    wsc_d = nc.dram_tensor("wsc", [NPAN, 128, 4096], BF16).ap()

    S = Sched()
    with ExitStack() as es:
        def sb(name, shape, dt):
            return es.enter_context(nc.sbuf_tensor("sb_" + name, shape, dt))

        xb = [sb(f"xb{i}", [128, NBLK, D], F32) for i in range(2)]
        grow = sb("grow", [128, nl + 1, D], F32)
        ring = [sb(f"ring{i}", [128, 8, 512], BF16) for i in range(RS)]
        hT = sb("hT", [128, 8, T], BF16)
        htok = [sb(f"htok{i}", [128, D], BF16) for i in range(NBLK)]
        junk = sb("junk", [128, D], BF16)
        stat = sb("stat", [128, 2 * NBLK, 4], F32)
        qp = sb("qp", [128, 8, T], BF16)
        kT = sb("kT", [128, 2, T], BF16)
        vpad = sb("vpad", [128, NBLK, 4, 128], BF16)
        kcar = [sb(f"kcar{i}", [128, 2, 128], BF16) for i in range(nA)]
        vcar = [sb(f"vcar{i}", [128, 4, 128], BF16) for i in range(nA)]
        szT = sb("szT", [128, 8, T], BF16)
        gT = sb("gT", [128, 8, T], BF16)
        NPB = 8
        pbuf = [sb(f"pbuf{i}", [128, 4, 128], BF16) for i in range(NPB)]
        tA = [sb(f"tA{i}", [128, 4, 128], F32) for i in range(2)]
        ucur = sb("ucur", [128, NBLK, D], BF16)
        ucar = [sb(f"ucar{i}", [128, D], BF16) for i in range(nB)]
        cbf = sb("cbf", [128, 128 + 256 + 2560 + 256 + 1024], BF16)
        esel = sb("esel", [128, nA * 2, 4], F32)
        psc = sb("psc", [128, nB * 8], F32)
        epst = sb("epst", [128, 1], F32)
        banks = [es.enter_context(nc.psum_tensor(f"bank{i}", [128, 512], F32)) for i in range(8)]

        ident = cbf[:, 0:128]
        mask2 = cbf[:, 128:384].rearrange("p (a q) -> p a q", a=2)
        amat = cbf[:, 384:2944].rearrange("p (w k t) -> p w k t", w=4, k=5)
        ones_h = cbf[:, 2944:3200].rearrange("p (a m) -> p a m", a=2)
        negm = cbf[:, 3200:4224].rearrange("p (a g q) -> p a g q", a=2, g=4)

        B_x = [[Buf(f"x{i}_{b}") for b in range(NBLK)] for i in range(2)]
        B_cbf, B_esel, B_psc, B_eps = Buf("cbf"), Buf("esel"), Buf("psc"), Buf("eps")
        B_grow = [Buf(f"grow{i}") for i in range(nl + 1)]
        B_ring = [Buf(f"ring{i}") for i in range(RS)]
        B_wsc = [Buf(f"wsc{i}") for i in range(NPAN)]
        B_hT = [Buf(f"hT{b}") for b in range(NBLK)]
        B_htok = [Buf(f"htok{i}") for i in range(NBLK)]
        B_stat = [Buf(f"stat{b}") for b in range(2 * NBLK)]
        B_qp = [Buf(f"qp{m}") for m in range(8)]
        B_pT = [[Buf(f"pT{b}_{h}") for h in range(2)] for b in range(NBLK)]
        B_kT = [Buf(f"kT{c}") for c in range(2)]
        B_v = [Buf(f"v{b}") for b in range(NBLK)]
        B_kcar = [Buf(f"kcar{i}") for i in range(nA)]
        B_vcar = [Buf(f"vcar{i}") for i in range(nA)]
        B_sz = [Buf(f"sz{m}") for m in range(8)]
        B_g = [[Buf(f"g{b}_{h}") for h in range(2)] for b in range(NBLK)]
        B_p = [Buf(f"p{i}") for i in range(NPB)]
        B_tA = [Buf(f"tA{i}") for i in range(2)]
        B_u = [Buf(f"u{b}") for b in range(NBLK)]
        B_ucar = [Buf(f"ucar{i}") for i in range(nB)]
        B_bank = [Buf(f"bank{i}") for i in range(8)]

        st = {"bank": 0, "p": 0, "tA": 0, "htok": 0, "alt": 0}

        def next_bank():
            i = st["bank"]
            st["bank"] = (i + 1) % 8
            return i

        def dma(eng, out, in_, reads, writes, dsem):
            return S.add(eng, lambda e: e.dma_start(out=out, in_=in_), reads=reads, writes=writes, dsem=dsem)

        dma("sp", cbf[:], cbf_d, [], [B_cbf], S.dsem("c0"))
        dma("sp", esel[:].rearrange("p a g -> p (a g)"), sinks_d, [], [B_esel], S.dsem("c1"))
        dma("sp", psc[:], pscale_d, [], [B_psc], S.dsem("c2"))
        for i in range(nl + 1):
            dma("sp", grow[:, i, :], gvec_d[i].partition_broadcast(128), [], [B_grow[i]], S.dsem(f"cg{i}"))
        S.add("dve", lambda e: e.memset(epst[:], EPS), writes=[B_eps])
        S.add("act", lambda e: e.activation(out=esel[:], in_=esel[:], func=AF.Exp), reads=[B_esel], writes=[B_esel])
        S.add("pool", lambda e: e.memset(vpad[:], 0.0), writes=B_v)
        for i in range(nA):
            S.add("pool", lambda e, i=i: e.memset(vcar[i][:], 0.0), writes=[B_vcar[i]])

        ring_sems = [S.dsem(f"rg{i}") for i in range(RS)]
        xsem = [S.dsem("xl0"), S.dsem("xl1")]
        osem = [S.dsem("os0"), S.dsem("os1")]

        def load_x(t):
            par = t % 2
            src = x_d[t * T:(t + 1) * T, :].rearrange("(b p) d -> p b d", p=128)
            dma("pool", xb[par][:], src, [], B_x[par], xsem[par])

        load_x(0)
        for p in range(NPAN):
            dma("pool", wsc_d[p], wpan_d[p], [], [B_wsc[p]], S.dsem(f"ws{p}"))
            if p == 6 and NT > 1:
                load_x(1)
        pan_ctr = [0]

        def load_panel(pidx, ncols=4096):
            slot = pan_ctr[0] % RS
            pan_ctr[0] += 1
            dst = ring[slot][:].rearrange("p a b -> p (a b)")
            dma("sp", dst[:, 0:ncols], wsc_d[pidx][:, 0:ncols], [B_wsc[pidx]], [B_ring[slot]], ring_sems[slot])
            return slot

        def norm_stats(xblk, Bx, b):
            S.add("act", lambda e: e.activation(out=junk[:], in_=xblk, func=AF.Square, scale=1.0 / 32.0,
                                                accum_out=stat[:, b, 0:1]),
                  reads=[Bx], writes=[B_stat[b]])
            S.add("act", lambda e: e.activation(out=stat[:, b, 1:2], in_=stat[:, b, 0:1], func=AF.Ln, bias=epst[:, 0:1]),
                  reads=[B_stat[b], B_eps], writes=[B_stat[b]])
            S.add("act", lambda e: e.activation(out=stat[:, b, 2:3], in_=stat[:, b, 1:2], func=AF.Exp, scale=-0.5),
                  reads=[B_stat[b]], writes=[B_stat[b]])

        def pre_norm(par, li, b):
            norm_stats(xb[par][:, b, :], B_x[par][b], b)
            if b > 0:
                pre_norm_h(par, li, b - 1)
            if b == NBLK - 1:
                pre_norm_h(par, li, b)

        def pre_norm_h(par, li, b):
            xblk = xb[par][:, b, :]
            S.add("dve", lambda e: e.scalar_tensor_tensor(
                out=htok[b][:], in0=xblk, scalar=stat[:, b, 2:3], in1=grow[:, li, :],
                op0=ALU.mult, op1=ALU.mult),
                reads=[B_x[par][b], B_stat[b], B_grow[li]], writes=[B_htok[b]])

        def tr_block(b, eng="dve"):
            bk = next_bank()
            pv = banks[bk][:].bitcast(BF16)

            def tr(e, b=b, pv=pv):
                for c in range(8):
                    i = e.transpose(out=pv[:, c * 128:(c + 1) * 128], in_=htok[b][:, c * 128:(c + 1) * 128],
                                    identity=ident)
                return i
            S.add("pe", tr, reads=[B_htok[b], B_cbf], writes=[B_bank[bk]], name=f"tr{b}")
            if eng == "act":
                S.add("act", lambda e, pv=pv, b=b: e.activation(
                    out=hT[:, :, b * 128:(b + 1) * 128], in_=pv.rearrange("p (c t) -> p c t", c=8), func=AF.Copy),
                    reads=[B_bank[bk]], writes=[B_hT[b]])
            else:
                S.add("dve", lambda e, pv=pv, b=b: e.tensor_copy(
                    out=hT[:, :, b * 128:(b + 1) * 128], in_=pv.rearrange("p (c t) -> p c t", c=8)),
                    reads=[B_bank[bk]], writes=[B_hT[b]])

        def fm_group(slot, mm, out_writes, evac):
            bk = next_bank()

            def f(e):
                for dc in range(8):
                    i = e.matmul(out=banks[bk][:], lhsT=ring[slot][:, dc, mm * 128:(mm + 1) * 128], rhs=hT[:, dc, :],
                                 start=(dc == 0), stop=(dc == 7))
                return i
            S.add("pe", f, reads=[B_ring[slot]] + B_hT, writes=[B_bank[bk]], name=f"fm{mm}")
            evac(bk)

        def evac_copy(dst, Bdst, eng):
            def ev(bk):
                if eng == "act":
                    S.add("act", lambda e: e.activation(out=dst, in_=banks[bk][:], func=AF.Copy),
                          reads=[B_bank[bk]], writes=Bdst)
                else:
                    S.add("dve", lambda e: e.tensor_copy(out=dst, in_=banks[bk][:]), reads=[B_bank[bk]], writes=Bdst)
            return ev

        def evac_silu(m):
            def ev(bk):
                S.add("act", lambda e: e.activation(out=szT[:, m, :], in_=banks[bk][:], func=AF.Silu),
                      reads=[B_bank[bk]], writes=[B_sz[m]])
            return ev

        def z_proj(pbase):
            for pi in range(2):
                slot = load_panel(pbase + pi)
                for mm in range(4):
                    fm_group(slot, mm, None, evac_silu(4 * pi + mm))

        def out_proj(par, pbase, after_block, mid_block, next_tr, tail, g_chunk_bufs=None):
            slots = [load_panel(pbase), load_panel(pbase + 1)]
            for b in range(NBLK):
                for half in range(2):
                    slot = slots[half]
                    bk = next_bank()
                    if b == 0 and half == 0 and g_chunk_bufs is not None:
                        for cc in range(8):
                            S.add("pe", lambda e, cc=cc, bk=bk, slot=slot: e.matmul(
                                out=banks[bk][:], lhsT=gT[:, cc, 0:128], rhs=ring[slot][:, cc, :],
                                start=(cc == 0), stop=(cc == 7)),
                                reads=[B_ring[slot], g_chunk_bufs[cc]], writes=[B_bank[bk]], name=f"oproj{half}_{b}_{cc}")
                    else:
                        def f(e, b=b, bk=bk, slot=slot):
                            for cc in range(8):
                                i = e.matmul(out=banks[bk][:], lhsT=gT[:, cc, b * 128:(b + 1) * 128], rhs=ring[slot][:, cc, :],
                                             start=(cc == 0), stop=(cc == 7))
                            return i
                        S.add("pe", f, reads=[B_ring[slot]] + B_g[b], writes=[B_bank[bk]], name=f"oproj{half}_{b}")
                    xs = xb[par][:, b, half * 512:(half + 1) * 512]
                    S.add("dve", lambda e, xs=xs, bk=bk: e.tensor_tensor(out=xs, in0=xs, in1=banks[bk][:], op=ALU.add),
                          reads=[B_bank[bk], B_x[par][b]], writes=[B_x[par][b]])
                    if half == 0:
                        mid_block(b)
                after_block(b)
                if b == 2:
                    next_tr(0)
            next_tr(1)
            next_tr(2)
            tail()

        def attn_layer(t, par, li, la, pbase, after_block, mid_block, next_tr, tail, pending, mid_hook, q_hook):
            first_tile = (t == 0) and seq_start
            slot = load_panel(pbase + 2)
            for b in range(NBLK):
                if b in pending:
                    tr_block(b)
                bk = next_bank()

                def f(e, b=b, bk=bk, slot=slot):
                    for dc in range(8):
                        i = e.matmul(out=banks[bk][:, 0:256], lhsT=hT[:, dc, b * 128:(b + 1) * 128],
                                     rhs=ring[slot][:, dc, 256:512], start=(dc == 0), stop=(dc == 7))
                    return i
                S.add("pe", f, reads=[B_ring[slot], B_hT[b]], writes=[B_bank[bk]], name=f"vproj{b}")

                def ev(e, b=b, bk=bk):
                    src = banks[bk][:, 0:256].rearrange("p (j d) -> p j d", j=4)
                    e.tensor_copy(out=vpad[:, b, 0:4:2, 0:64], in_=src[:, 0:4:2, :])
                    return e.tensor_copy(out=vpad[:, b, 1:4:2, 64:128], in_=src[:, 1:4:2, :])
                S.add("dve", ev, reads=[B_bank[bk]], writes=[B_v[b]])
            if mid_hook is not None:
                mid_hook()
            for c in range(2):
                fm_group(slot, c, None, evac_copy(kT[:, c, :], [B_kT[c]], "dve"))
            for pi in range(2):
                slot = load_panel(pbase + pi)
                for mm in range(4):
                    m = 4 * pi + mm
                    eng = "act" if (m % 2 == 0) else "dve"
                    fm_group(slot, mm, None, evac_copy(qp[:, m, :], [B_qp[m]], eng))
            if q_hook is not None:
                q_hook()
            z_proj(pbase + 3)

            kus = [(b, pr, jj) for b in range(NBLK) for pr in range(2) for jj in range(2)]
            staged = {}

            def stage_qk(n):
                b, pr, jj = kus[n]
                has_prev = not (first_tile and b == 0)
                kbs = ([0] if has_prev else []) + [1]
                ptiles = []
                for kb in kbs:
                    bk = 2 * (n % 2) + kb
                    if kb == 0:
                        if b == 0:
                            kap, Bk = kcar[la][jj * 64:(jj + 1) * 64, pr, :], B_kcar[la]
                        else:
                            kap, Bk = kT[jj * 64:(jj + 1) * 64, pr, (b - 1) * 128:b * 128], B_kT[pr]
                    else:
                        kap, Bk = kT[jj * 64:(jj + 1) * 64, pr, b * 128:(b + 1) * 128], B_kT[pr]
                    qap = qp[jj * 64:(jj + 1) * 64, 4 * pr:4 * pr + 4, b * 128:(b + 1) * 128]
                    def fqk(e, bk=bk, kap=kap, qap=qap, kb=kb):
                        o = banks[bk][:].rearrange("p (g q) -> p g q", g=4)
                        e.matmul(out=o, lhsT=kap, rhs=qap, start=True, stop=False)
                        return e.matmul(out=o, lhsT=ident, rhs=negm[:, kb, :, :], start=False, stop=True)
                    S.add("pe", fqk, reads=[Bk, B_cbf] + B_qp[4 * pr:4 * pr + 4], writes=[B_bank[bk]],
                          name=f"qk{b}_{pr}_{jj}{kb}")
                    pi_ = st["p"]
                    st["p"] = (pi_ + 1) % NPB
                    S.add("act", lambda e, bk=bk, pi_=pi_: e.activation(
                        out=pbuf[pi_][:], in_=banks[bk][:].rearrange("p (g q) -> p g q", g=4), func=AF.Exp, scale=0.125),
                        reads=[B_bank[bk]], writes=[B_p[pi_]])
                    ptiles.append((kb, pi_))
                staged[n] = ptiles

            def stage_pv(n):
                b, pr, jj = kus[n]
                m = n // 2
                ptiles = staged.pop(n)
                bo, bd = 4 + 2 * (m % 2), 5 + 2 * (m % 2)
                j = 2 * pr + jj
                np_ = len(ptiles)

                def fo(e):
                    for i_, (kb, pi_) in enumerate(ptiles):
                        if kb == 0:
                            vap = vcar[la][:, j, :] if b == 0 else vpad[:, b - 1, j, :]
                        else:
                            vap = vpad[:, b, j, :]
                        i = e.matmul(out=banks[bo][:].rearrange("p (g q) -> p g q", g=4), lhsT=vap, rhs=pbuf[pi_][:],
                                     start=(jj == 0 and i_ == 0), stop=(jj == 1 and i_ == np_ - 1))
                    return i
                rv = [B_p[pi_] for (_, pi_) in ptiles] + [B_v[b]] + ([B_vcar[la]] if b == 0 else [B_v[b - 1]])
                S.add("pe", fo, reads=rv, writes=[B_bank[bo]], name=f"pv{b}_{pr}_{jj}")

                def fd(e):
                    for i_, (kb, pi_) in enumerate(ptiles):
                        i = e.matmul(out=banks[bd][:].rearrange("p (g q) -> p g q", g=4), lhsT=ones_h[:, jj, :], rhs=pbuf[pi_][:],
                                     start=(jj == 0 and i_ == 0), stop=(jj == 1 and i_ == np_ - 1))
                    return i
                S.add("pe", fd, reads=[B_p[pi_] for (_, pi_) in ptiles] + [B_cbf], writes=[B_bank[bd]], name=f"den{b}_{pr}_{jj}")

            def stage_norm(m):
                b, pr = m // 2, m % 2
                bo, bd = 4 + 2 * (m % 2), 5 + 2 * (m % 2)
                ti = m % 2
                S.add("dve", lambda e: e.tensor_tensor(
                    out=tA[ti][:], in0=banks[bd][:].rearrange("p (g q) -> p g q", g=4),
                    in1=esel[:, la * 2 + pr, :].unsqueeze(2).to_broadcast([128, 4, 128]), op=ALU.add),
                    reads=[B_bank[bd], B_esel], writes=[B_tA[ti]])
                S.add("act", lambda e: e.activation(out=tA[ti][:], in_=tA[ti][:], func=AF.Ln), reads=[B_tA[ti]], writes=[B_tA[ti]])
                S.add("act", lambda e: e.activation(out=tA[ti][:], in_=tA[ti][:], func=AF.Exp, scale=-1.0),
                      reads=[B_tA[ti]], writes=[B_tA[ti]])
                gsl = gT[:, 4 * pr:4 * pr + 4, b * 128:(b + 1) * 128]
                S.add("dve", lambda e: e.tensor_tensor(
                    out=gsl, in0=banks[bo][:].rearrange("p (g q) -> p g q", g=4), in1=tA[ti][:], op=ALU.mult),
                    reads=[B_bank[bo], B_tA[ti]], writes=[B_g[b][pr]])
                S.add("dve", lambda e: e.tensor_tensor(
                    out=gsl, in0=gsl, in1=szT[:, 4 * pr:4 * pr + 4, b * 128:(b + 1) * 128], op=ALU.mult),
                    reads=[B_g[b][pr]] + B_sz[4 * pr:4 * pr + 4], writes=[B_g[b][pr]])

            NK = len(kus)
            stage_qk(0)
            for n in range(NK):
                if n + 1 < NK:
                    stage_qk(n + 1)
                stage_pv(n)
                if n % 2 == 0 and n >= 2:
                    stage_norm(n // 2 - 1)
            stage_norm(NK // 2 - 1)
            S.add("act", lambda e: e.activation(out=kcar[la][:], in_=kT[:, :, 3 * 128:4 * 128], func=AF.Copy),
                  reads=B_kT, writes=[B_kcar[la]])
            S.add("act", lambda e: e.activation(out=vcar[la][:], in_=vpad[:, 3, :, :], func=AF.Copy),
                  reads=[B_v[3]], writes=[B_vcar[la]])
            out_proj(par, pbase + 5, after_block, mid_block, next_tr, tail)

        B_gc = [Buf(f"gc{c}") for c in range(8)]
        B_tmp = [Buf(f"tmp{c}") for c in range(8)]
        u32 = ucur[:].rearrange("p b d -> p (b d)").bitcast(F32)
        p32 = qp[:].rearrange("p c t -> p (c t)").bitcast(F32)

        def pool_layer(t, par, li, lb, pbase, after_block, mid_block, next_tr, tail, pending, mid_hook, q_hook):
            first_tile = (t == 0) and seq_start
            for half in range(2):
                slot = load_panel(pbase + half)
                for b in range(NBLK):
                    if half == 0 and b in pending:
                        tr_block(b)
                    bk = next_bank()

                    def f(e, b=b, bk=bk, slot=slot):
                        for dc in range(8):
                            i = e.matmul(out=banks[bk][:], lhsT=hT[:, dc, b * 128:(b + 1) * 128], rhs=ring[slot][:, dc, :],
                                         start=(dc == 0), stop=(dc == 7))
                        return i
                    S.add("pe", f, reads=[B_ring[slot], B_hT[b]], writes=[B_bank[bk]], name=f"uproj{half}_{b}")
                    dst = ucur[:, b, half * 512:(half + 1) * 512]
                    if (b + half) % 2 == 0:
                        S.add("act", lambda e, dst=dst, bk=bk: e.activation(out=dst, in_=banks[bk][:], func=AF.Copy),
                              reads=[B_bank[bk]], writes=[B_u[b]])
                    else:
                        S.add("dve", lambda e, dst=dst, bk=bk: e.tensor_copy(out=dst, in_=banks[bk][:]),
                              reads=[B_bank[bk]], writes=[B_u[b]])
                if half == 0 and mid_hook is not None:
                    mid_hook()
            if q_hook is not None:
                q_hook()
            for hb in range(2):
                for b in range(NBLK):
                    first_blk = first_tile and b == 0
                    bk = next_bank()

                    def f(e, b=b, hb=hb, bk=bk, first_blk=first_blk):
                        for q4 in range(4):
                            cc = 4 * hb + q4
                            w = cc // 2
                            o = banks[bk][:, q4 * 128:(q4 + 1) * 128]
                            ucc = ucur[:, b, cc * 128:(cc + 1) * 128]
                            if first_blk:
                                e.matmul(out=o, lhsT=ucc, rhs=amat[:, w, 2, :], start=True, stop=False)
                                e.matmul(out=o, lhsT=ucc, rhs=amat[:, w, 3, :], start=False, stop=False)
                                i = e.matmul(out=o, lhsT=ucc, rhs=amat[:, w, 4, :], start=False, stop=True)
                            else:
                                upv = ucar[lb][:, cc * 128:(cc + 1) * 128] if b == 0 else ucur[:, b - 1, cc * 128:(cc + 1) * 128]
                                e.matmul(out=o, lhsT=upv, rhs=amat[:, w, 1, :], start=True, stop=False)
                                i = e.matmul(out=o, lhsT=ucc, rhs=amat[:, w, 0, :], start=False, stop=True)
                        return i
                    rd = [B_u[b], B_cbf] + ([] if first_blk else ([B_ucar[lb]] if b == 0 else [B_u[b - 1]]))
                    S.add("pe", f, reads=rd, writes=[B_bank[bk]], name=f"pool{b}_{hb}")
                    dst = qp[:, 4 * hb:4 * hb + 4, b * 128:(b + 1) * 128]
                    src = banks[bk][:].rearrange("p (c t) -> p c t", c=4)
                    if b % 2 == 0:
                        S.add("act", lambda e, dst=dst, src=src: e.activation(out=dst, in_=src, func=AF.Copy),
                              reads=[B_bank[bk]], writes=[B_pT[b][hb]])
                    else:
                        S.add("dve", lambda e, dst=dst, src=src: e.tensor_copy(out=dst, in_=src),
                              reads=[B_bank[bk]], writes=[B_pT[b][hb]])
            S.add("act", lambda e: e.activation(out=ucar[lb][:], in_=ucur[:, 3, :], func=AF.Copy), reads=[B_u[3]], writes=[B_ucar[lb]])
            slot = load_panel(pbase + 4, ncols=2048)
            wm = ring[slot][:].rearrange("p a b -> p (a b)")[:, 0:2048].rearrange("p (gk d) -> p gk d", gk=8)
            for dch in range(8):
                g = dch // 2
                bk = next_bank()

                def f(e, dch=dch, g=g, bk=bk):
                    for k in range(2):
                        i = e.matmul(out=banks[bk][:], lhsT=wm[:, 2 * g + k, (dch % 2) * 128:(dch % 2 + 1) * 128],
                                     rhs=qp[:, 2 * g + k, :], start=(k == 0), stop=(k == 1))
                    return i
                S.add("pe", f, reads=[B_ring[slot]] + [B_pT[b][g // 2] for b in range(NBLK)], writes=[B_bank[bk]], name=f"mix{dch}")
                if dch < 4:
                    tmp = u32[:, dch * 512:(dch + 1) * 512]
                    alias = [B_u[dch]]
                else:
                    tmp = p32[:, (dch - 4) * 512:(dch - 3) * 512]
                    alias = [B_pT[b][(dch - 4) // 2] for b in range(NBLK)]
                if dch % 2 == 0:
                    S.add("act", lambda e, dch=dch, bk=bk, tmp=tmp: e.activation(
                        out=tmp, in_=banks[bk][:], func=AF.Copy, scale=psc[:, lb * 8 + dch:lb * 8 + dch + 1]),
                        reads=[B_bank[bk], B_psc], writes=[B_tmp[dch]] + alias)
                else:
                    S.add("dve", lambda e, dch=dch, bk=bk, tmp=tmp: e.tensor_scalar(
                        out=tmp, in0=banks[bk][:], scalar1=psc[:, lb * 8 + dch:lb * 8 + dch + 1], scalar2=None,
                        op0=ALU.mult),
                        reads=[B_bank[bk], B_psc], writes=[B_tmp[dch]] + alias)
            for pi in range(2):
                slot = load_panel(pbase + 2 + pi)
                for mm in range(4):
                    m = 4 * pi + mm
                    fm_group(slot, mm, None, evac_silu(m))
                    if m < 4:
                        tmp = u32[:, m * 512:(m + 1) * 512]
                        alias = [B_u[m]]
                    else:
                        tmp = p32[:, (m - 4) * 512:(m - 3) * 512]
                        alias = [B_pT[b][(m - 4) // 2] for b in range(NBLK)]
                    S.add("dve", lambda e, m=m, tmp=tmp: e.tensor_tensor(
                        out=gT[:, m, :], in0=tmp, in1=szT[:, m, :], op=ALU.mult),
                        reads=[B_tmp[m], B_sz[m]] + alias, writes=[B_g[b][m // 4] for b in range(NBLK)] + [B_gc[m]])
            out_proj(par, pbase + 5, after_block, mid_block, next_tr, tail, g_chunk_bufs=B_gc)

        store_ops = []

        def final_norm(par, b):
            xblk = xb[par][:, b, :]
            sb_ = NBLK + b
            S.add("act", lambda e: e.activation(out=junk[:], in_=xblk, func=AF.Square, scale=1.0 / 32.0,
                                                accum_out=stat[:, sb_, 0:1]),
                  reads=[B_x[par][b]], writes=[B_stat[sb_]])
            S.add("act", lambda e: e.activation(out=stat[:, sb_, 1:2], in_=stat[:, sb_, 0:1], func=AF.Ln, bias=epst[:, 0:1]),
                  reads=[B_stat[sb_], B_eps], writes=[B_stat[sb_]])
            S.add("act", lambda e: e.activation(out=stat[:, sb_, 2:3], in_=stat[:, sb_, 1:2], func=AF.Exp, scale=-0.5),
                  reads=[B_stat[sb_]], writes=[B_stat[sb_]])
            S.add("dve", lambda e: e.scalar_tensor_tensor(
                out=xblk, in0=xblk, scalar=stat[:, sb_, 2:3], in1=grow[:, nl, :], op0=ALU.mult, op1=ALU.mult),
                reads=[B_x[par][b], B_stat[sb_], B_grow[nl]], writes=[B_x[par][b]])

        for b in range(NBLK):
            pre_norm(0, 0, b)
        pending = list(range(NBLK))
        deferred_store = [None]

        def finish_tile(tp_):
            parp = tp_ % 2
            if do_final:
                for b in range(NBLK):
                    final_norm(parp, b)
            dst = out_d[tp_ * T:(tp_ + 1) * T, :].rearrange("(b p) d -> p b d", p=128)
            so = S.add("pool", lambda e: e.dma_start(out=dst, in_=xb[parp][:]),
                       reads=B_x[parp], writes=[], dsem=osem[parp])
            store_ops.append(so)
            if tp_ + 2 < NT:
                load_x(tp_ + 2)

        for t in range(NT):
            par = t % 2
            la = lb = 0
            for li, kind in enumerate(kinds):
                S.ctx = f"t{t}.L{li}."
                last = (li + 1 == nl)
                if not last:
                    def after_block(b, par=par, li=li):
                        norm_stats(xb[par][:, b, :], B_x[par][b], b)

                    def tail(par=par, li=li):
                        pre_norm_h(par, li + 1, NBLK - 1)

                    def mid_block(b, par=par, li=li):
                        if b > 0:
                            pre_norm_h(par, li + 1, b - 1)

                    def next_tr(b):
                        tr_block(b, "act" if b == 0 else "dve")
                    nxt_pending = [3]
                else:
                    def after_block(b):
                        pass

                    def mid_block(b):
                        pass

                    def tail():
                        pass
                    if t + 1 < NT:
                        def next_tr(b):
                            tr_block(b)
                            if b == 2:
                                tr_block(3)
                        nxt_pending = []
                    else:
                        def next_tr(b):
                            pass
                        nxt_pending = []
                mid_hook = None
                if last and t + 1 < NT:
                    def mid_hook(par=par):
                        for b in range(NBLK):
                            pre_norm(1 - par, 0, b)
                q_hook = None
                if li == 0 and t > 0:
                    def q_hook(t=t):
                        finish_tile(t - 1)
                if kind == "A":
                    attn_layer(t, par, li, la, 7 * li, after_block, mid_block, next_tr, tail, pending, mid_hook, q_hook)
                    la += 1
                else:
                    pool_layer(t, par, li, lb, 7 * li, after_block, mid_block, next_tr, tail, pending, mid_hook, q_hook)
                    lb += 1
                pending = nxt_pending
        finish_tile(NT - 1)
        if _DEBUG_PE_LOG is not None:
            S.pe_log = _DEBUG_PE_LOG
        S.emit(nc, final_waits=[("pool", o) for o in store_ops[-2:]])
    return nc


def _panel(W):
    return np.ascontiguousarray(W.reshape(8, 128, 512).transpose(1, 0, 2).reshape(128, 4096))


def _qperm():
    perm = np.empty(1024, dtype=np.int64)
    for m in range(8):
        pr, g = m // 4, m % 4
        for p in range(128):
            h = (2 * pr + (1 if p >= 64 else 0)) * 4 + g
            perm[m * 128 + p] = h * 64 + (p % 64)
    return perm


def _attn_panels(w_in, w_out):
    perm = _qperm()
    Wq = w_in[:, 0:1024][:, perm]
    Wkv = w_in[:, 1024:1536]
    Wz = w_in[:, 1536:2560][:, perm]
    Wo = w_out[perm, :]
    return [_panel(Wq[:, :512]), _panel(Wq[:, 512:]), _panel(Wkv), _panel(Wz[:, :512]), _panel(Wz[:, 512:]),
            _panel(Wo[:, :512]), _panel(Wo[:, 512:])]


def _pool_panels(w_in, w_mix, w_out):
    mix = np.zeros((128, 4096), dtype=np.float32)
    mix[:, :2048] = w_mix.reshape(4, 2, 128, 256).transpose(2, 0, 1, 3).reshape(128, 2048)
    return [_panel(w_in[:, 0:512]), _panel(w_in[:, 512:1024]), _panel(w_in[:, 1024:1536]), _panel(w_in[:, 1536:2048]),
            mix, _panel(w_out[:, :512]), _panel(w_out[:, 512:])]


def _sinks_sel(sinks):
    o = np.empty((128, 2, 4), dtype=np.float32)
    for pr in range(2):
        for g in range(4):
            o[:64, pr, g] = sinks[(2 * pr) * 4 + g]
            o[64:, pr, g] = sinks[(2 * pr + 1) * 4 + g]
    return o.reshape(128, 8)


def _consts():
    c = np.zeros((128, 128 + 256 + 2560 + 256 + 1024), dtype=np.float32)
    c[:, 0:128] = np.eye(128)
    k = np.arange(128)[:, None]
    q = np.arange(128)[None, :]
    c[:, 128:256] = (k > q)
    c[:, 256:384] = (k <= q)
    A = np.zeros((128, 4, 5, 128), dtype=np.float64)
    for wi, w in enumerate(WINDOWS):
        for t in range(128):
            for i in range(w):
                tp = t - i
                if tp >= 0:
                    A[tp, wi, 0, t] += 1.0 / w
                else:
                    A[128 + tp, wi, 1, t] += 1.0 / w
            A[t, wi, 0, t] -= 1.0
            cnt = min(t + 1, w)
            for i in range(cnt):
                A[t - i, wi, 2, t] += 1.0 / cnt
            A[t, wi, 2, t] -= 1.0
    full = A[:, :, 2, :].copy()
    hi = full.astype(ml_dtypes.bfloat16).astype(np.float64)
    mid = (full - hi).astype(ml_dtypes.bfloat16).astype(np.float64)
    lo = full - hi - mid
    A[:, :, 2, :], A[:, :, 3, :], A[:, :, 4, :] = hi, mid, lo
    c[:, 384:2944] = A.reshape(128, 2560)
    c[:, 2944:2944 + 64] = 1.0
    c[:, 2944 + 128 + 64:2944 + 256] = 1.0
    nm = np.zeros((128, 2, 4, 128), dtype=np.float32)
    nm[:, 0, :, :] = np.where(k > q, 0.0, -30000.0)[:, None, :]
    nm[:, 1, :, :] = np.where(k <= q, 0.0, -30000.0)[:, None, :]
    c[:, 3200:4224] = nm.reshape(128, 1024)
    return c.astype(ml_dtypes.bfloat16)


_PROG_CACHE = {}
_DEBUG_PE_LOG = None


def _get_prog(S_tok, kinds, do_final):
    key = (S_tok, tuple(kinds), do_final)
    if key not in _PROG_CACHE:
        _PROG_CACHE[key] = build_program(S_tok, list(kinds), do_final)
    return _PROG_CACHE[key]


def _run(x, layer_ids, do_final, norm_g, attn_w_in, attn_sinks, attn_w_out, pool_w_in, pool_w_mix, pool_scale,
         pool_w_out, final_g):
    Bsz, S_tok, _ = x.shape
    kinds = ["A" if i % 2 == 0 else "B" for i in layer_ids]
    pans, gv, sk, ps = [], [], [], []
    for i in layer_ids:
        j = i // 2
        gv.append(norm_g[i])
        if i % 2 == 0:
            pans += _attn_panels(attn_w_in[j], attn_w_out[j])
            sk.append(_sinks_sel(attn_sinks[j]))
        else:
            pans += _pool_panels(pool_w_in[j], pool_w_mix[j], pool_w_out[j])
            ps.append(np.ascontiguousarray(pool_scale[j].reshape(8, 128).T))
    gv.append(final_g)
    if not sk:
        sk.append(np.zeros((128, 8), np.float32))
    if not ps:
        ps.append(np.zeros((128, 8), np.float32))
    wpan = np.ascontiguousarray(np.stack(pans, 0), dtype=np.float32)
    gvec = np.ascontiguousarray(np.stack(gv, 0), dtype=np.float32)
    sinks = np.ascontiguousarray(np.concatenate(sk, 1), dtype=np.float32)
    pscale = np.ascontiguousarray(np.concatenate(ps, 1), dtype=np.float32)
    cbf = _consts()
    nc = _get_prog(S_tok, kinds, do_final)
    in_maps = [{"x": np.ascontiguousarray(x[c]), "wpan": wpan, "gvec": gvec, "sinks": sinks, "pscale": pscale,
                "cbf": cbf} for c in range(Bsz)]
    res = run_bass_kernel_spmd(nc, in_maps, core_ids=list(range(Bsz)))
    return np.stack([np.asarray(res.results[c]["out"]) for c in range(Bsz)], 0)


FUSED = True


def kernel(x, norm_g, attn_w_in, attn_sinks, attn_w_out, pool_w_in, pool_w_mix, pool_scale, pool_w_out, final_g):
    x = np.asarray(x, dtype=np.float32)
    args = [np.asarray(a, dtype=np.float32) for a in
            (norm_g, attn_w_in, attn_sinks, attn_w_out, pool_w_in, pool_w_mix, pool_scale, pool_w_out, final_g)]
    if FUSED:
        return _run(x, [0, 1, 2, 3], True, *args)
    for i in range(4):
        x = _run(x, [i], i == 3, *args)
    return x
```

```python
from contextlib import ExitStack

import numpy as np
import ml_dtypes
import concourse.bass as bass
import concourse.mybir as mybir
from concourse.bass_utils import run_bass_kernel_spmd

F32 = mybir.dt.float32
BF16 = mybir.dt.bfloat16
ALU = mybir.AluOpType
AF = mybir.ActivationFunctionType

ENGS = ("pe", "act", "dve", "pool", "sp")

D = 1024
T = 512
NBLK = 4
WINDOWS = (2, 4, 8, 16)
EPS = 1e-6
N_CORES = 8


class Buf:
    __slots__ = ("name", "writer", "readers")

    def __init__(self, name):
        self.name = name
        self.writer = None
        self.readers = {}


class Op:
    __slots__ = ("eng", "fn", "deps", "signal", "sem", "inc", "ticket", "name")


class _Count:
    def __init__(self, e):
        self.e = e
        self.n = 0

    def matmul(self, *a, **k):
        self.n += 1
        return self.e.matmul(*a, **k)

    def transpose(self, *a, **k):
        self.n += 1
        return self.e.transpose(*a, **k)


class Sched:
    def __init__(self):
        self.q = {k: [] for k in ENGS}
        self.dsems = []
        self.ctx = ""
        self.pe_log = None

    def dsem(self, name):
        self.dsems.append(name)
        return name

    def add(self, eng, fn, reads=(), writes=(), dsem=None, name=""):
        name = self.ctx + name
        op = Op()
        op.eng = eng
        op.fn = fn
        op.signal = dsem is not None
        op.name = name
        op.sem = dsem if dsem is not None else eng
        op.inc = 16 if dsem is not None else 1
        op.ticket = None
        deps = []
        for b in reads:
            if b.writer is not None:
                deps.append(b.writer)
        for b in writes:
            if b.readers:
                deps.extend(b.readers.values())
            elif b.writer is not None:
                deps.append(b.writer)
        out = []
        for d in deps:
            if d is op:
                continue
            if d.sem == "pe" and eng == "pe":
                continue
            out.append(d)
            d.signal = True
        op.deps = out
        for b in reads:
            b.readers[op.sem] = op
        for b in writes:
            b.writer = op
            b.readers = {}
        self.q[eng].append(op)
        return op

    def emit(self, nc, final_waits=()):
        cnt = {}
        for eng in ENGS:
            for op in self.q[eng]:
                if op.signal:
                    cnt[op.sem] = cnt.get(op.sem, 0) + op.inc
                    op.ticket = cnt[op.sem]
        names = list(ENGS) + list(self.dsems)
        with ExitStack() as es:
            sems = {n: es.enter_context(nc.semaphore("s_" + n)) for n in names}
            block = es.enter_context(nc.Block())

            def run(engkey, e):
                waited = {}
                for op in self.q[engkey]:
                    need = {}
                    for d in op.deps:
                        if need.get(d.sem, 0) < d.ticket:
                            need[d.sem] = d.ticket
                    for s, v in need.items():
                        if waited.get(s, 0) < v:
                            e.wait_ge(sems[s], v)
                            waited[s] = v
                    if self.pe_log is not None and engkey == "pe":
                        cp = _Count(e)
                        inst = op.fn(cp)
                        self.pe_log.append((op.name, cp.n))
                    else:
                        inst = op.fn(e)
                    if op.signal:
                        inst.then_inc(sems[op.sem], op.inc)
                for (ek, op) in final_waits:
                    if ek == engkey:
                        e.wait_ge(sems[op.sem], op.ticket)

            @block.tensor
            def _(e):
                run("pe", e)

            @block.scalar
            def _(e):
                run("act", e)

            @block.vector
            def _(e):
                run("dve", e)

            @block.gpsimd
            def _(e):
                run("pool", e)

            @block.sync
            def _(e):
                run("sp", e)


def build_program(S_tok, kinds, do_final, seq_start=True, RS=6):
    nl = len(kinds)
    nA = max(1, sum(1 for k in kinds if k == "A"))
    nB = max(1, sum(1 for k in kinds if k == "B"))
    NT = S_tok // T
    NPAN = 7 * nl
    nc = bass.Bass("TRN2", target_bir_lowering=False)
    x_d = nc.dram_tensor("x", [S_tok, D], F32, kind="ExternalInput").ap()
    wpan_d = nc.dram_tensor("wpan", [NPAN, 128, 4096], F32, kind="ExternalInput").ap()
    gvec_d = nc.dram_tensor("gvec", [nl + 1, D], F32, kind="ExternalInput").ap()
    sinks_d = nc.dram_tensor("sinks", [128, nA * 8], F32, kind="ExternalInput").ap()
    pscale_d = nc.dram_tensor("pscale", [128, nB * 8], F32, kind="ExternalInput").ap()
    cbf_d = nc.dram_tensor("cbf", [128, 128 + 256 + 2560 + 256 + 1024], BF16, kind="ExternalInput").ap()
    out_d = nc.dram_tensor("out", [S_tok, D], F32, kind="ExternalOutput").ap()
    wsc_d = nc.dram_tensor("wsc", [NPAN, 128, 4096], BF16).ap()

    S = Sched()
    with ExitStack() as es:
        def sb(name, shape, dt):
            return es.enter_context(nc.sbuf_tensor("sb_" + name, shape, dt))

        xb = [sb(f"xb{i}", [128, NBLK, D], F32) for i in range(2)]
        grow = sb("grow", [128, nl + 1, D], F32)
        ring = [sb(f"ring{i}", [128, 8, 512], BF16) for i in range(RS)]
        hT = sb("hT", [128, 8, T], BF16)
        htok = [sb(f"htok{i}", [128, D], BF16) for i in range(NBLK)]
        junk = sb("junk", [128, D], BF16)
        stat = sb("stat", [128, 2 * NBLK, 4], F32)
        qp = sb("qp", [128, 8, T], BF16)
        kT = sb("kT", [128, 2, T], BF16)
        vpad = sb("vpad", [128, NBLK, 4, 128], BF16)
        kcar = [sb(f"kcar{i}", [128, 2, 128], BF16) for i in range(nA)]
        vcar = [sb(f"vcar{i}", [128, 4, 128], BF16) for i in range(nA)]
        szT = sb("szT", [128, 8, T], BF16)
        gT = sb("gT", [128, 8, T], BF16)
        NPB = 8
        pbuf = [sb(f"pbuf{i}", [128, 4, 128], BF16) for i in range(NPB)]
        tA = [sb(f"tA{i}", [128, 4, 128], F32) for i in range(2)]
        ucur = sb("ucur", [128, NBLK, D], BF16)
        ucar = [sb(f"ucar{i}", [128, D], BF16) for i in range(nB)]
        cbf = sb("cbf", [128, 128 + 256 + 2560 + 256 + 1024], BF16)
        esel = sb("esel", [128, nA * 2, 4], F32)
        psc = sb("psc", [128, nB * 8], F32)
        epst = sb("epst", [128, 1], F32)
        banks = [es.enter_context(nc.psum_tensor(f"bank{i}", [128, 512], F32)) for i in range(8)]

        qp4 = qp[:].rearrange("p c t -> p (c t)").rearrange("p (b c q) -> p b c q", b=NBLK, c=8)
        ident = cbf[:, 0:128]
        mask2 = cbf[:, 128:384].rearrange("p (a q) -> p a q", a=2)
        amat = cbf[:, 384:2944].rearrange("p (w k t) -> p w k t", w=4, k=5)
        ones_h = cbf[:, 2944:3200].rearrange("p (a m) -> p a m", a=2)
        negm = cbf[:, 3200:4224].rearrange("p (a g q) -> p a g q", a=2, g=4)

        B_x = [[Buf(f"x{i}_{b}") for b in range(NBLK)] for i in range(2)]
        B_cbf, B_esel, B_psc, B_eps = Buf("cbf"), Buf("esel"), Buf("psc"), Buf("eps")
        B_grow = [Buf(f"grow{i}") for i in range(nl + 1)]
        B_ring = [Buf(f"ring{i}") for i in range(RS)]
        B_wsc = [Buf(f"wsc{i}") for i in range(NPAN)]
        B_hT = [Buf(f"hT{b}") for b in range(NBLK)]
        B_htok = [Buf(f"htok{i}") for i in range(NBLK)]
        B_stat = [Buf(f"stat{b}") for b in range(2 * NBLK)]
        B_qp = [Buf(f"qp{m}") for m in range(8)]
        B_pT = [[Buf(f"pT{b}_{h}") for h in range(2)] for b in range(NBLK)]
        B_kT = [Buf(f"kT{c}") for c in range(2)]
        B_v = [Buf(f"v{b}") for b in range(NBLK)]
        B_kcar = [Buf(f"kcar{i}") for i in range(nA)]
        B_vcar = [Buf(f"vcar{i}") for i in range(nA)]
        B_sz = [Buf(f"sz{m}") for m in range(8)]
        B_g = [[Buf(f"g{b}_{h}") for h in range(2)] for b in range(NBLK)]
        B_p = [Buf(f"p{i}") for i in range(NPB)]
        B_tA = [Buf(f"tA{i}") for i in range(2)]
        B_u = [Buf(f"u{b}") for b in range(NBLK)]
        B_ucar = [Buf(f"ucar{i}") for i in range(nB)]
        B_bank = [Buf(f"bank{i}") for i in range(8)]

        st = {"bank": 0, "p": 0, "tA": 0, "htok": 0, "alt": 0}

        def next_bank():
            i = st["bank"]
            st["bank"] = (i + 1) % 8
            return i

        def dma(eng, out, in_, reads, writes, dsem):
            return S.add(eng, lambda e: e.dma_start(out=out, in_=in_), reads=reads, writes=writes, dsem=dsem)

        dma("sp", cbf[:], cbf_d, [], [B_cbf], S.dsem("c0"))
        dma("sp", esel[:].rearrange("p a g -> p (a g)"), sinks_d, [], [B_esel], S.dsem("c1"))
        dma("sp", psc[:], pscale_d, [], [B_psc], S.dsem("c2"))
        for i in range(nl + 1):
            dma("sp", grow[:, i, :], gvec_d[i].partition_broadcast(128), [], [B_grow[i]], S.dsem(f"cg{i}"))
        S.add("dve", lambda e: e.memset(epst[:], EPS), writes=[B_eps])
        S.add("act", lambda e: e.activation(out=esel[:], in_=esel[:], func=AF.Exp), reads=[B_esel], writes=[B_esel])
        S.add("pool", lambda e: e.memset(vpad[:], 0.0), writes=B_v)
        for i in range(nA):
            S.add("pool", lambda e, i=i: e.memset(vcar[i][:], 0.0), writes=[B_vcar[i]])

        ring_sems = [S.dsem(f"rg{i}") for i in range(RS)]
        xsem = [S.dsem("xl0"), S.dsem("xl1")]
        osem = [S.dsem("os0"), S.dsem("os1")]

        def load_x(t):
            par = t % 2
            src = x_d[t * T:(t + 1) * T, :].rearrange("(b p) d -> p b d", p=128)
            dma("pool", xb[par][:], src, [], B_x[par], xsem[par])

        load_x(0)
        pp_order = ([2, 0, 1] if kinds[0] == "A" else [0, 1, 2]) + list(range(3, NPAN))
        for i_, p in enumerate(pp_order):
            dma("pool", wsc_d[p], wpan_d[p], [], [B_wsc[p]], S.dsem(f"ws{p}"))
            if i_ == 6 and NT > 1:
                load_x(1)
        pan_ctr = [0]

        def load_panel(pidx, ncols=4096):
            slot = pan_ctr[0] % RS
            pan_ctr[0] += 1
            dst = ring[slot][:].rearrange("p a b -> p (a b)")
            dma("sp", dst[:, 0:ncols], wsc_d[pidx][:, 0:ncols], [B_wsc[pidx]], [B_ring[slot]], ring_sems[slot])
            return slot

        def norm_stats(xblk, Bx, b):
            S.add("act", lambda e: e.activation(out=junk[:], in_=xblk, func=AF.Square, scale=1.0 / 32.0,
                                                accum_out=stat[:, b, 0:1]),
                  reads=[Bx], writes=[B_stat[b]])
            S.add("act", lambda e: e.activation(out=stat[:, b, 1:2], in_=stat[:, b, 0:1], func=AF.Ln, bias=epst[:, 0:1]),
                  reads=[B_stat[b], B_eps], writes=[B_stat[b]])
            S.add("act", lambda e: e.activation(out=stat[:, b, 2:3], in_=stat[:, b, 1:2], func=AF.Exp, scale=-0.5),
                  reads=[B_stat[b]], writes=[B_stat[b]])

        def pre_norm(par, li, b):
            norm_stats(xb[par][:, b, :], B_x[par][b], b)
            if b > 0:
                pre_norm_h(par, li, b - 1)
            if b == NBLK - 1:
                pre_norm_h(par, li, b)

        def pre_norm_h(par, li, b):
            xblk = xb[par][:, b, :]
            S.add("dve", lambda e: e.scalar_tensor_tensor(
                out=htok[b][:], in0=xblk, scalar=stat[:, b, 2:3], in1=grow[:, li, :],
                op0=ALU.mult, op1=ALU.mult),
                reads=[B_x[par][b], B_stat[b], B_grow[li]], writes=[B_htok[b]])

        def tr_block(b, eng="dve"):
            bk = next_bank()
            pv = banks[bk][:].bitcast(BF16)

            def tr(e, b=b, pv=pv):
                for c in range(8):
                    i = e.transpose(out=pv[:, c * 128:(c + 1) * 128], in_=htok[b][:, c * 128:(c + 1) * 128],
                                    identity=ident)
                return i
            S.add("pe", tr, reads=[B_htok[b], B_cbf], writes=[B_bank[bk]], name=f"tr{b}")
            if eng == "act":
                S.add("act", lambda e, pv=pv, b=b: e.activation(
                    out=hT[:, :, b * 128:(b + 1) * 128], in_=pv.rearrange("p (c t) -> p c t", c=8), func=AF.Copy),
                    reads=[B_bank[bk]], writes=[B_hT[b]])
            else:
                S.add("dve", lambda e, pv=pv, b=b: e.tensor_copy(
                    out=hT[:, :, b * 128:(b + 1) * 128], in_=pv.rearrange("p (c t) -> p c t", c=8)),
                    reads=[B_bank[bk]], writes=[B_hT[b]])

        def fm_group(slot, mm, out_writes, evac):
            bk = next_bank()

            def f(e):
                for dc in range(8):
                    i = e.matmul(out=banks[bk][:], lhsT=ring[slot][:, dc, mm * 128:(mm + 1) * 128], rhs=hT[:, dc, :],
                                 start=(dc == 0), stop=(dc == 7))
                return i
            S.add("pe", f, reads=[B_ring[slot]] + B_hT, writes=[B_bank[bk]], name=f"fm{mm}")
            evac(bk)

        def evac_copy(dst, Bdst, eng, blocked=False):
            def ev(bk):
                src = banks[bk][:].rearrange("p (b q) -> p b q", b=NBLK) if blocked else banks[bk][:]
                if eng == "act":
                    S.add("act", lambda e: e.activation(out=dst, in_=src, func=AF.Copy),
                          reads=[B_bank[bk]], writes=Bdst)
                else:
                    S.add("dve", lambda e: e.tensor_copy(out=dst, in_=src), reads=[B_bank[bk]], writes=Bdst)
            return ev

        def evac_silu(m):
            def ev(bk):
                S.add("act", lambda e: e.activation(out=szT[:, m, :], in_=banks[bk][:], func=AF.Silu),
                      reads=[B_bank[bk]], writes=[B_sz[m]])
            return ev

        def z_proj(pbase):
            for pi in range(2):
                slot = load_panel(pbase + pi)
                for mm in range(4):
                    fm_group(slot, mm, None, evac_silu(4 * pi + mm))

        def out_proj(par, pbase, after_block, mid_block, next_tr, tail, g_chunk_bufs=None):
            slots = [load_panel(pbase), load_panel(pbase + 1)]
            for b in range(NBLK):
                for half in range(2):
                    slot = slots[half]
                    bk = next_bank()
                    if b == 0 and half == 0 and g_chunk_bufs is not None:
                        for cc in range(8):
                            S.add("pe", lambda e, cc=cc, bk=bk, slot=slot: e.matmul(
                                out=banks[bk][:], lhsT=gT[:, cc, 0:128], rhs=ring[slot][:, cc, :],
                                start=(cc == 0), stop=(cc == 7)),
                                reads=[B_ring[slot], g_chunk_bufs[cc]], writes=[B_bank[bk]], name=f"oproj{half}_{b}_{cc}")
                    else:
                        def f(e, b=b, bk=bk, slot=slot):
                            for cc in range(8):
                                i = e.matmul(out=banks[bk][:], lhsT=gT[:, cc, b * 128:(b + 1) * 128], rhs=ring[slot][:, cc, :],
                                             start=(cc == 0), stop=(cc == 7))
                            return i
                        S.add("pe", f, reads=[B_ring[slot]] + B_g[b], writes=[B_bank[bk]], name=f"oproj{half}_{b}")
                    xs = xb[par][:, b, half * 512:(half + 1) * 512]
                    S.add("dve", lambda e, xs=xs, bk=bk: e.tensor_tensor(out=xs, in0=xs, in1=banks[bk][:], op=ALU.add),
                          reads=[B_bank[bk], B_x[par][b]], writes=[B_x[par][b]])
                    if half == 0:
                        mid_block(b)
                after_block(b)
                if b == 2:
                    next_tr(0)
            next_tr(1)
            next_tr(2)
            tail()

        def attn_layer(t, par, li, la, pbase, after_block, mid_block, next_tr, tail, pending, mid_hook, q_hook):
            first_tile = (t == 0) and seq_start
            slot = load_panel(pbase + 2)
            for b in range(NBLK):
                if b in pending:
                    tr_block(b)
                bk = next_bank()

                def f(e, b=b, bk=bk, slot=slot):
                    for dc in range(8):
                        i = e.matmul(out=banks[bk][:, 0:256], lhsT=hT[:, dc, b * 128:(b + 1) * 128],
                                     rhs=ring[slot][:, dc, 256:512], start=(dc == 0), stop=(dc == 7))
                    return i
                S.add("pe", f, reads=[B_ring[slot], B_hT[b]], writes=[B_bank[bk]], name=f"vproj{b}")

                def ev(e, b=b, bk=bk):
                    src = banks[bk][:, 0:256].rearrange("p (j d) -> p j d", j=4)
                    e.tensor_copy(out=vpad[:, b, 0:4:2, 0:64], in_=src[:, 0:4:2, :])
                    return e.tensor_copy(out=vpad[:, b, 1:4:2, 64:128], in_=src[:, 1:4:2, :])
                S.add("dve", ev, reads=[B_bank[bk]], writes=[B_v[b]])
            if mid_hook is not None:
                mid_hook()
            for c in range(2):
                fm_group(slot, c, None, evac_copy(kT[:, c, :], [B_kT[c]], "dve"))
            for pi in range(2):
                slot = load_panel(pbase + pi)
                for mm in range(4):
                    m = 4 * pi + mm
                    eng = "act" if (m % 2 == 0) else "dve"
                    fm_group(slot, mm, None, evac_copy(qp4[:, :, m, :], [B_qp[m]], eng, blocked=True))
            if q_hook is not None:
                q_hook()
            z_proj(pbase + 3)

            kus = [(b, pr, jj) for b in range(NBLK) for pr in range(2) for jj in range(2)]
            staged = {}

            def stage_qk(n):
                b, pr, jj = kus[n]
                has_prev = not (first_tile and b == 0)
                kbs = ([0] if has_prev else []) + [1]
                ptiles = []
                for kb in kbs:
                    bk = 2 * (n % 2) + kb
                    if kb == 0:
                        if b == 0:
                            kap, Bk = kcar[la][jj * 64:(jj + 1) * 64, pr, :], B_kcar[la]
                        else:
                            kap, Bk = kT[jj * 64:(jj + 1) * 64, pr, (b - 1) * 128:b * 128], B_kT[pr]
                    else:
                        kap, Bk = kT[jj * 64:(jj + 1) * 64, pr, b * 128:(b + 1) * 128], B_kT[pr]
                    qap = qp4[jj * 64:(jj + 1) * 64, b, 4 * pr:4 * pr + 4, :]
                    def fqk(e, bk=bk, kap=kap, qap=qap, kb=kb):
                        o = banks[bk][:].rearrange("p (g q) -> p g q", g=4)
                        e.matmul(out=o, lhsT=kap, rhs=qap, start=True, stop=False)
                        return e.matmul(out=o, lhsT=ident, rhs=negm[:, kb, :, :], start=False, stop=True)
                    S.add("pe", fqk, reads=[Bk, B_cbf] + B_qp[4 * pr:4 * pr + 4], writes=[B_bank[bk]],
                          name=f"qk{b}_{pr}_{jj}{kb}")
                    pi_ = st["p"]
                    st["p"] = (pi_ + 1) % NPB
                    S.add("act", lambda e, bk=bk, pi_=pi_: e.activation(
                        out=pbuf[pi_][:], in_=banks[bk][:].rearrange("p (g q) -> p g q", g=4), func=AF.Exp, scale=0.125),
                        reads=[B_bank[bk]], writes=[B_p[pi_]])
                    ptiles.append((kb, pi_))
                staged[n] = ptiles

            def stage_pv(n):
                b, pr, jj = kus[n]
                m = n // 2
                ptiles = staged.pop(n)
                bo, bd = 4 + 2 * (m % 2), 5 + 2 * (m % 2)
                j = 2 * pr + jj
                np_ = len(ptiles)

                def fo(e):
                    for i_, (kb, pi_) in enumerate(ptiles):
                        if kb == 0:
                            vap = vcar[la][:, j, :] if b == 0 else vpad[:, b - 1, j, :]
                        else:
                            vap = vpad[:, b, j, :]
                        i = e.matmul(out=banks[bo][:].rearrange("p (g q) -> p g q", g=4), lhsT=vap, rhs=pbuf[pi_][:],
                                     start=(jj == 0 and i_ == 0), stop=(jj == 1 and i_ == np_ - 1))
                    return i
                rv = [B_p[pi_] for (_, pi_) in ptiles] + [B_v[b]] + ([B_vcar[la]] if b == 0 else [B_v[b - 1]])
                S.add("pe", fo, reads=rv, writes=[B_bank[bo]], name=f"pv{b}_{pr}_{jj}")

                def fd(e):
                    for i_, (kb, pi_) in enumerate(ptiles):
                        i = e.matmul(out=banks[bd][:].rearrange("p (g q) -> p g q", g=4), lhsT=ones_h[:, jj, :], rhs=pbuf[pi_][:],
                                     start=(jj == 0 and i_ == 0), stop=(jj == 1 and i_ == np_ - 1))
                    return i
                S.add("pe", fd, reads=[B_p[pi_] for (_, pi_) in ptiles] + [B_cbf], writes=[B_bank[bd]], name=f"den{b}_{pr}_{jj}")

            def stage_norm(m):
                b, pr = m // 2, m % 2
                bo, bd = 4 + 2 * (m % 2), 5 + 2 * (m % 2)
                ti = m % 2
                S.add("dve", lambda e: e.tensor_tensor(
                    out=tA[ti][:], in0=banks[bd][:].rearrange("p (g q) -> p g q", g=4),
                    in1=esel[:, la * 2 + pr, :].unsqueeze(2).to_broadcast([128, 4, 128]), op=ALU.add),
                    reads=[B_bank[bd], B_esel], writes=[B_tA[ti]])
                S.add("act", lambda e: e.activation(out=tA[ti][:], in_=tA[ti][:], func=AF.Ln), reads=[B_tA[ti]], writes=[B_tA[ti]])
                S.add("act", lambda e: e.activation(out=tA[ti][:], in_=tA[ti][:], func=AF.Exp, scale=-1.0),
                      reads=[B_tA[ti]], writes=[B_tA[ti]])
                gsl = gT[:, 4 * pr:4 * pr + 4, b * 128:(b + 1) * 128]
                S.add("dve", lambda e: e.tensor_tensor(
                    out=gsl, in0=banks[bo][:].rearrange("p (g q) -> p g q", g=4), in1=tA[ti][:], op=ALU.mult),
                    reads=[B_bank[bo], B_tA[ti]], writes=[B_g[b][pr]])
                S.add("dve", lambda e: e.tensor_tensor(
                    out=gsl, in0=gsl, in1=szT[:, 4 * pr:4 * pr + 4, b * 128:(b + 1) * 128], op=ALU.mult),
                    reads=[B_g[b][pr]] + B_sz[4 * pr:4 * pr + 4], writes=[B_g[b][pr]])

            NK = len(kus)
            stage_qk(0)
            for n in range(NK):
                if n + 1 < NK:
                    stage_qk(n + 1)
                stage_pv(n)
                if n % 2 == 0 and n >= 2:
                    stage_norm(n // 2 - 1)
            stage_norm(NK // 2 - 1)
            S.add("pool", lambda e: e.tensor_copy(out=kcar[la][:], in_=kT[:, :, 3 * 128:4 * 128]),
                  reads=B_kT, writes=[B_kcar[la]])
            S.add("pool", lambda e: e.tensor_copy(out=vcar[la][:], in_=vpad[:, 3, :, :]),
                  reads=[B_v[3]], writes=[B_vcar[la]])
            out_proj(par, pbase + 5, after_block, mid_block, next_tr, tail)

        B_gc = [Buf(f"gc{c}") for c in range(8)]
        B_tmp = [Buf(f"tmp{c}") for c in range(8)]
        u32 = ucur[:].rearrange("p b d -> p (b d)").bitcast(F32)
        p32 = qp[:].rearrange("p c t -> p (c t)").bitcast(F32)

        def pool_layer(t, par, li, lb, pbase, after_block, mid_block, next_tr, tail, pending, mid_hook, q_hook):
            first_tile = (t == 0) and seq_start
            for half in range(2):
                slot = load_panel(pbase + half)
                for b in range(NBLK):
                    if half == 0 and b in pending:
                        tr_block(b)
                    bk = next_bank()

                    def f(e, b=b, bk=bk, slot=slot):
                        for dc in range(8):
                            i = e.matmul(out=banks[bk][:], lhsT=hT[:, dc, b * 128:(b + 1) * 128], rhs=ring[slot][:, dc, :],
                                         start=(dc == 0), stop=(dc == 7))
                        return i
                    S.add("pe", f, reads=[B_ring[slot], B_hT[b]], writes=[B_bank[bk]], name=f"uproj{half}_{b}")
                    dst = ucur[:, b, half * 512:(half + 1) * 512]
                    if (b + half) % 2 == 0:
                        S.add("act", lambda e, dst=dst, bk=bk: e.activation(out=dst, in_=banks[bk][:], func=AF.Copy),
                              reads=[B_bank[bk]], writes=[B_u[b]])
                    else:
                        S.add("dve", lambda e, dst=dst, bk=bk: e.tensor_copy(out=dst, in_=banks[bk][:]),
                              reads=[B_bank[bk]], writes=[B_u[b]])
                if half == 0 and mid_hook is not None:
                    mid_hook()
            if q_hook is not None:
                q_hook()
            for hb in range(2):
                for b in range(NBLK):
                    first_blk = first_tile and b == 0
                    bk = next_bank()

                    def f(e, b=b, hb=hb, bk=bk, first_blk=first_blk):
                        for q4 in range(4):
                            cc = 4 * hb + q4
                            w = cc // 2
                            o = banks[bk][:, q4 * 128:(q4 + 1) * 128]
                            ucc = ucur[:, b, cc * 128:(cc + 1) * 128]
                            if first_blk:
                                e.matmul(out=o, lhsT=ucc, rhs=amat[:, w, 2, :], start=True, stop=False)
                                e.matmul(out=o, lhsT=ucc, rhs=amat[:, w, 3, :], start=False, stop=False)
                                i = e.matmul(out=o, lhsT=ucc, rhs=amat[:, w, 4, :], start=False, stop=True)
                            else:
                                upv = ucar[lb][:, cc * 128:(cc + 1) * 128] if b == 0 else ucur[:, b - 1, cc * 128:(cc + 1) * 128]
                                e.matmul(out=o, lhsT=upv, rhs=amat[:, w, 1, :], start=True, stop=False)
                                i = e.matmul(out=o, lhsT=ucc, rhs=amat[:, w, 0, :], start=False, stop=True)
                        return i
                    rd = [B_u[b], B_cbf] + ([] if first_blk else ([B_ucar[lb]] if b == 0 else [B_u[b - 1]]))
                    S.add("pe", f, reads=rd, writes=[B_bank[bk]], name=f"pool{b}_{hb}")
                    dst = qp[:, 4 * hb:4 * hb + 4, b * 128:(b + 1) * 128]
                    src = banks[bk][:].rearrange("p (c t) -> p c t", c=4)
                    if b % 2 == 0:
                        S.add("act", lambda e, dst=dst, src=src: e.activation(out=dst, in_=src, func=AF.Copy),
                              reads=[B_bank[bk]], writes=[B_pT[b][hb]])
                    else:
                        S.add("dve", lambda e, dst=dst, src=src: e.tensor_copy(out=dst, in_=src),
                              reads=[B_bank[bk]], writes=[B_pT[b][hb]])
            S.add("pool", lambda e: e.tensor_copy(out=ucar[lb][:], in_=ucur[:, 3, :]), reads=[B_u[3]], writes=[B_ucar[lb]])
            slot = load_panel(pbase + 4, ncols=2048)
            wm = ring[slot][:].rearrange("p a b -> p (a b)")[:, 0:2048].rearrange("p (gk d) -> p gk d", gk=8)
            for dch in range(8):
                g = dch // 2
                bk = next_bank()

                def f(e, dch=dch, g=g, bk=bk):
                    for k in range(2):
                        i = e.matmul(out=banks[bk][:], lhsT=wm[:, 2 * g + k, (dch % 2) * 128:(dch % 2 + 1) * 128],
                                     rhs=qp[:, 2 * g + k, :], start=(k == 0), stop=(k == 1))
                    return i
                S.add("pe", f, reads=[B_ring[slot]] + [B_pT[b][g // 2] for b in range(NBLK)], writes=[B_bank[bk]], name=f"mix{dch}")
                if dch < 4:
                    tmp = u32[:, dch * 512:(dch + 1) * 512]
                    alias = [B_u[dch]]
                else:
                    tmp = p32[:, (dch - 4) * 512:(dch - 3) * 512]
                    alias = [B_pT[b][(dch - 4) // 2] for b in range(NBLK)]
                if dch % 2 == 0:
                    S.add("act", lambda e, dch=dch, bk=bk, tmp=tmp: e.activation(
                        out=tmp, in_=banks[bk][:], func=AF.Copy, scale=psc[:, lb * 8 + dch:lb * 8 + dch + 1]),
                        reads=[B_bank[bk], B_psc], writes=[B_tmp[dch]] + alias)
                else:
                    S.add("dve", lambda e, dch=dch, bk=bk, tmp=tmp: e.tensor_scalar(
                        out=tmp, in0=banks[bk][:], scalar1=psc[:, lb * 8 + dch:lb * 8 + dch + 1], scalar2=None,
                        op0=ALU.mult),
                        reads=[B_bank[bk], B_psc], writes=[B_tmp[dch]] + alias)
            for pi in range(2):
                slot = load_panel(pbase + 2 + pi)
                for mm in range(4):
                    m = 4 * pi + mm
                    fm_group(slot, mm, None, evac_silu(m))
                    if m < 4:
                        tmp = u32[:, m * 512:(m + 1) * 512]
                        alias = [B_u[m]]
                    else:
                        tmp = p32[:, (m - 4) * 512:(m - 3) * 512]
                        alias = [B_pT[b][(m - 4) // 2] for b in range(NBLK)]
                    S.add("dve", lambda e, m=m, tmp=tmp: e.tensor_tensor(
                        out=gT[:, m, :], in0=tmp, in1=szT[:, m, :], op=ALU.mult),
                        reads=[B_tmp[m], B_sz[m]] + alias, writes=[B_g[b][m // 4] for b in range(NBLK)] + [B_gc[m]])
            out_proj(par, pbase + 5, after_block, mid_block, next_tr, tail, g_chunk_bufs=B_gc)

        store_ops = []

        def final_norm(par, b):
            xblk = xb[par][:, b, :]
            sb_ = NBLK + b
            S.add("act", lambda e: e.activation(out=junk[:], in_=xblk, func=AF.Square, scale=1.0 / 32.0,
                                                accum_out=stat[:, sb_, 0:1]),
                  reads=[B_x[par][b]], writes=[B_stat[sb_]])
            S.add("act", lambda e: e.activation(out=stat[:, sb_, 1:2], in_=stat[:, sb_, 0:1], func=AF.Ln, bias=epst[:, 0:1]),
                  reads=[B_stat[sb_], B_eps], writes=[B_stat[sb_]])
            S.add("act", lambda e: e.activation(out=stat[:, sb_, 2:3], in_=stat[:, sb_, 1:2], func=AF.Exp, scale=-0.5),
                  reads=[B_stat[sb_]], writes=[B_stat[sb_]])
            S.add("dve", lambda e: e.scalar_tensor_tensor(
                out=xblk, in0=xblk, scalar=stat[:, sb_, 2:3], in1=grow[:, nl, :], op0=ALU.mult, op1=ALU.mult),
                reads=[B_x[par][b], B_stat[sb_], B_grow[nl]], writes=[B_x[par][b]])

        for b in range(NBLK):
            pre_norm(0, 0, b)
        pending = list(range(NBLK))
        deferred_store = [None]

        def finish_tile(tp_):
            parp = tp_ % 2
            if do_final:
                for b in range(NBLK):
                    final_norm(parp, b)
            dst = out_d[tp_ * T:(tp_ + 1) * T, :].rearrange("(b p) d -> p b d", p=128)
            so = S.add("pool", lambda e: e.dma_start(out=dst, in_=xb[parp][:]),
                       reads=B_x[parp], writes=[], dsem=osem[parp])
            store_ops.append(so)
            if tp_ + 2 < NT:
                load_x(tp_ + 2)

        for t in range(NT):
            par = t % 2
            la = lb = 0
            for li, kind in enumerate(kinds):
                S.ctx = f"t{t}.L{li}."
                last = (li + 1 == nl)
                if not last:
                    def after_block(b, par=par, li=li):
                        norm_stats(xb[par][:, b, :], B_x[par][b], b)

                    def tail(par=par, li=li):
                        pre_norm_h(par, li + 1, NBLK - 1)

                    def mid_block(b, par=par, li=li):
                        if b > 0:
                            pre_norm_h(par, li + 1, b - 1)

                    def next_tr(b):
                        tr_block(b, "act" if b == 0 else "dve")
                    nxt_pending = [3]
                else:
                    def after_block(b):
                        pass

                    def mid_block(b):
                        pass

                    def tail():
                        pass
                    if t + 1 < NT:
                        def next_tr(b):
                            tr_block(b)
                            if b == 2:
                                tr_block(3)
                        nxt_pending = []
                    else:
                        def next_tr(b):
                            pass
                        nxt_pending = []
                mid_hook = None
                if last and t + 1 < NT:
                    def mid_hook(par=par):
                        for b in range(NBLK):
                            pre_norm(1 - par, 0, b)
                q_hook = None
                if li == 0 and t > 0:
                    def q_hook(t=t):
                        finish_tile(t - 1)
                if kind == "A":
                    attn_layer(t, par, li, la, 7 * li, after_block, mid_block, next_tr, tail, pending, mid_hook, q_hook)
                    la += 1
                else:
                    pool_layer(t, par, li, lb, 7 * li, after_block, mid_block, next_tr, tail, pending, mid_hook, q_hook)
                    lb += 1
                pending = nxt_pending
        finish_tile(NT - 1)
        if _DEBUG_PE_LOG is not None:
            S.pe_log = _DEBUG_PE_LOG
        S.emit(nc, final_waits=[("pool", o) for o in store_ops[-2:]])
    return nc


def _panel(W):
    return np.ascontiguousarray(W.reshape(8, 128, 512).transpose(1, 0, 2).reshape(128, 4096))


def _qperm():
    perm = np.empty(1024, dtype=np.int64)
    for m in range(8):
        pr, g = m // 4, m % 4
        for p in range(128):
            h = (2 * pr + (1 if p >= 64 else 0)) * 4 + g
            perm[m * 128 + p] = h * 64 + (p % 64)
    return perm


def _attn_panels(w_in, w_out):
    perm = _qperm()
    Wq = w_in[:, 0:1024][:, perm]
    Wkv = w_in[:, 1024:1536]
    Wz = w_in[:, 1536:2560][:, perm]
    Wo = w_out[perm, :]
    return [_panel(Wq[:, :512]), _panel(Wq[:, 512:]), _panel(Wkv), _panel(Wz[:, :512]), _panel(Wz[:, 512:]),
            _panel(Wo[:, :512]), _panel(Wo[:, 512:])]


def _pool_panels(w_in, w_mix, w_out):
    mix = np.zeros((128, 4096), dtype=np.float32)
    mix[:, :2048] = w_mix.reshape(4, 2, 128, 256).transpose(2, 0, 1, 3).reshape(128, 2048)
    return [_panel(w_in[:, 0:512]), _panel(w_in[:, 512:1024]), _panel(w_in[:, 1024:1536]), _panel(w_in[:, 1536:2048]),
            mix, _panel(w_out[:, :512]), _panel(w_out[:, 512:])]


def _sinks_sel(sinks):
    o = np.empty((128, 2, 4), dtype=np.float32)
    for pr in range(2):
        for g in range(4):
            o[:64, pr, g] = sinks[(2 * pr) * 4 + g]
            o[64:, pr, g] = sinks[(2 * pr + 1) * 4 + g]
    return o.reshape(128, 8)


def _consts():
    c = np.zeros((128, 128 + 256 + 2560 + 256 + 1024), dtype=np.float32)
    c[:, 0:128] = np.eye(128)
    k = np.arange(128)[:, None]
    q = np.arange(128)[None, :]
    c[:, 128:256] = (k > q)
    c[:, 256:384] = (k <= q)
    A = np.zeros((128, 4, 5, 128), dtype=np.float64)
    for wi, w in enumerate(WINDOWS):
        for t in range(128):
            for i in range(w):
                tp = t - i
                if tp >= 0:
                    A[tp, wi, 0, t] += 1.0 / w
                else:
                    A[128 + tp, wi, 1, t] += 1.0 / w
            A[t, wi, 0, t] -= 1.0
            cnt = min(t + 1, w)
            for i in range(cnt):
                A[t - i, wi, 2, t] += 1.0 / cnt
            A[t, wi, 2, t] -= 1.0
    full = A[:, :, 2, :].copy()
    hi = full.astype(ml_dtypes.bfloat16).astype(np.float64)
    mid = (full - hi).astype(ml_dtypes.bfloat16).astype(np.float64)
    lo = full - hi - mid
    A[:, :, 2, :], A[:, :, 3, :], A[:, :, 4, :] = hi, mid, lo
    c[:, 384:2944] = A.reshape(128, 2560)
    c[:, 2944:2944 + 64] = 1.0
    c[:, 2944 + 128 + 64:2944 + 256] = 1.0
    nm = np.zeros((128, 2, 4, 128), dtype=np.float32)
    nm[:, 0, :, :] = np.where(k > q, 0.0, -30000.0)[:, None, :]
    nm[:, 1, :, :] = np.where(k <= q, 0.0, -30000.0)[:, None, :]
    c[:, 3200:4224] = nm.reshape(128, 1024)
    return c.astype(ml_dtypes.bfloat16)


_PROG_CACHE = {}
_DEBUG_PE_LOG = None


def _get_prog(S_tok, kinds, do_final):
    key = (S_tok, tuple(kinds), do_final)
    if key not in _PROG_CACHE:
        _PROG_CACHE[key] = build_program(S_tok, list(kinds), do_final)
    return _PROG_CACHE[key]


def _run(x, layer_ids, do_final, norm_g, attn_w_in, attn_sinks, attn_w_out, pool_w_in, pool_w_mix, pool_scale,
         pool_w_out, final_g):
    Bsz, S_tok, _ = x.shape
    kinds = ["A" if i % 2 == 0 else "B" for i in layer_ids]
    pans, gv, sk, ps = [], [], [], []
    for i in layer_ids:
        j = i // 2
        gv.append(norm_g[i])
        if i % 2 == 0:
            pans += _attn_panels(attn_w_in[j], attn_w_out[j])
            sk.append(_sinks_sel(attn_sinks[j]))
        else:
            pans += _pool_panels(pool_w_in[j], pool_w_mix[j], pool_w_out[j])
            ps.append(np.ascontiguousarray(pool_scale[j].reshape(8, 128).T))
    gv.append(final_g)
    if not sk:
        sk.append(np.zeros((128, 8), np.float32))
    if not ps:
        ps.append(np.zeros((128, 8), np.float32))
    wpan = np.ascontiguousarray(np.stack(pans, 0), dtype=np.float32)
    gvec = np.ascontiguousarray(np.stack(gv, 0), dtype=np.float32)
    sinks = np.ascontiguousarray(np.concatenate(sk, 1), dtype=np.float32)
    pscale = np.ascontiguousarray(np.concatenate(ps, 1), dtype=np.float32)
    cbf = _consts()
    nc = _get_prog(S_tok, kinds, do_final)
    in_maps = [{"x": np.ascontiguousarray(x[c]), "wpan": wpan, "gvec": gvec, "sinks": sinks, "pscale": pscale,
                "cbf": cbf} for c in range(Bsz)]
    res = run_bass_kernel_spmd(nc, in_maps, core_ids=list(range(Bsz)))
    return np.stack([np.asarray(res.results[c]["out"]) for c in range(Bsz)], 0)


FUSED = True


def kernel(x, norm_g, attn_w_in, attn_sinks, attn_w_out, pool_w_in, pool_w_mix, pool_scale, pool_w_out, final_g):
    x = np.asarray(x, dtype=np.float32)
    args = [np.asarray(a, dtype=np.float32) for a in
            (norm_g, attn_w_in, attn_sinks, attn_w_out, pool_w_in, pool_w_mix, pool_scale, pool_w_out, final_g)]
    if FUSED:
        return _run(x, [0, 1, 2, 3], True, *args)
    for i in range(4):
        x = _run(x, [i], i == 3, *args)
    return x
```
